# Optimizing a Trainium2 kernel written in Bass

```python
import jax, jax.numpy as jnp
from jax import lax
import numpy as np

D_MODEL = 2048
BATCH = 2
SEQ = 16384
DEPTH = 4
DEC_BATCH = 16
DEC_SEQ = 64
PAST_LEN = 1024

CHUNK = 64
N_A_LAYERS = DEPTH // 2
N_B_LAYERS = DEPTH - N_A_LAYERS
EXPAND = 128
H_A = D_MODEL // EXPAND
K_A = EXPAND
V_A = D_MODEL // H_A
H_B = 16
D_HEAD_B = D_MODEL // H_B
QBLOCK = 128
D_PLE = 256
FORGET_BIAS_INIT = 3.0
EPS = 1e-6

kernel_name = "yoco_hgrn2_fox_stream_step"


def rmsnorm(x, g):
    xf = x.astype(jnp.float32)
    y = xf * lax.rsqrt(jnp.mean(xf * xf, axis=-1, keepdims=True) + EPS)
    return (y * g.astype(jnp.float32)).astype(x.dtype)


def hgrn2_chunk(S, inp):
    q, k, g, v = inp
    L = q.shape[2]
    G = jnp.cumsum(g, axis=2)
    causal = jnp.tril(jnp.ones((L, L), dtype=bool))
    diff = G[:, :, :, None, :] - G[:, :, None, :, :]
    decay = jnp.exp(jnp.where(causal[:, :, None], diff, -jnp.inf))
    A = jnp.einsum('bhtk,bhsk,bhtsk->bhts', q, k, decay)
    o = jnp.einsum('bhts,bhsv->bhtv', A, v) + jnp.einsum('bhtk,bhkv->bhtv', q * jnp.exp(G), S)
    G_last = G[:, :, -1:, :]
    S_new = jnp.exp(G_last[:, :, 0, :])[..., None] * S + jnp.einsum(
        'bhsk,bhsv->bhkv', k * jnp.exp(G_last - G), v)
    return S_new, o


def hgrn2_layer(x, S0, g_norm, w_in, lb, g_out, w_out):
    B, T, _ = x.shape
    z = rmsnorm(x, g_norm) @ w_in
    q, fl, i, gate = jnp.split(z, 4, axis=-1)
    lbf = lb.astype(jnp.float32)
    f = lbf + (1.0 - lbf) * jax.nn.sigmoid(fl.astype(jnp.float32))
    g = jnp.log(f)
    k = 1.0 - f
    q = jax.nn.silu(q.astype(jnp.float32))
    L = min(CHUNK, T)
    nc = T // L

    def blocks(a, d):
        a = a.astype(jnp.float32).reshape(B, nc, L, H_A, d)
        return jnp.transpose(a, (1, 0, 3, 2, 4))

    S_fin, o = lax.scan(hgrn2_chunk, S0.astype(jnp.float32),
                        (blocks(q, K_A), blocks(k, K_A), blocks(g, K_A), blocks(i, V_A)))
    o = jnp.transpose(o, (1, 0, 3, 2, 4)).reshape(B, T, H_A, V_A)
    o = o * lax.rsqrt(jnp.mean(o * o, axis=-1, keepdims=True) + EPS)
    o = o.reshape(B, T, D_MODEL) * g_out.astype(jnp.float32)
    y = (o * jax.nn.silu(gate.astype(jnp.float32))).astype(x.dtype) @ w_out
    return x + y, S_fin


def shared_kv(x, g_kv, w_kv, b_f):
    B, T, _ = x.shape
    z = rmsnorm(x, g_kv) @ w_kv
    k = z[..., :D_MODEL].reshape(B, T, H_B, D_HEAD_B)
    v = z[..., D_MODEL:2 * D_MODEL].reshape(B, T, H_B, D_HEAD_B)
    logf = jax.nn.log_sigmoid(z[..., 2 * D_MODEL:].astype(jnp.float32) + b_f.astype(jnp.float32))
    return k, v, logf


def fox_attention(q, k_all, v_all, dq, dkT, q_off):
    B, T, H, Dh = q.shape
    Lk = k_all.shape[1]
    k_pos = jnp.arange(Lk)
    blk = min(QBLOCK, T)
    nb = T // blk
    qb = jnp.moveaxis(q.reshape(B, nb, blk, H, Dh), 1, 0)
    dqb = jnp.moveaxis(dq.reshape(B, nb, blk, H), 1, 0)
    pos = (q_off + jnp.arange(T)).reshape(nb, blk)
    scale = Dh ** -0.5

    def one_block(args):
        qi, dqi, pi = args
        s = jnp.einsum('bqhd,bkhd->bhqk', qi, k_all, preferred_element_type=jnp.float32) * scale
        s = s + jnp.transpose(dqi, (0, 2, 1))[..., None] - dkT[:, :, None, :]
        mask = k_pos[None, :] <= pi[:, None]
        s = jnp.where(mask, s, -jnp.inf)
        pr = jax.nn.softmax(s, axis=-1)
        return jnp.einsum('bhqk,bkhd->bqhd', pr.astype(v_all.dtype), v_all)

    o = lax.map(one_block, (qb, dqb, pos))
    return jnp.moveaxis(o, 0, 1).reshape(B, T, H, Dh)


def fox_layer(x, k_all, v_all, dq, dkT, q_off, g_norm, w_in, w_out):
    B, T, _ = x.shape
    z = rmsnorm(x, g_norm) @ w_in
    q, gate = jnp.split(z, 2, axis=-1)
    o = fox_attention(q.reshape(B, T, H_B, D_HEAD_B), k_all, v_all, dq, dkT, q_off)
    o = o.reshape(B, T, D_MODEL).astype(jnp.float32) * jax.nn.silu(gate.astype(jnp.float32))
    return x + o.astype(x.dtype) @ w_out


def ple_add(x, p_i, w_pin, g_pg, w_pg):
    e = (p_i @ w_pin).astype(jnp.float32)
    gate = jax.nn.sigmoid((rmsnorm(x, g_pg) @ w_pg).astype(jnp.float32))
    return x + (gate * e).astype(x.dtype)


def trunk(x, p, hgrn_s0, past_k, past_v, past_logf,
          g_norm_a, w_in_a, lb_logits, g_out_a, w_out_a, g_kv, w_kv, b_f,
          g_norm_b, w_in_b, w_out_b, w_ple_in, g_ple, w_ple_gate, g_final):
    lbs = jnp.cumsum(jax.nn.softmax(lb_logits.astype(jnp.float32), axis=0), axis=0)
    lbs = lbs - lbs[:1]
    hgrn_states = []
    for layer in range(DEPTH):
        if layer < N_A_LAYERS:
            x, s = hgrn2_layer(x, hgrn_s0[layer], g_norm_a[layer], w_in_a[layer], lbs[layer],
                               g_out_a[layer], w_out_a[layer])
            hgrn_states.append(s)
        else:
            j = layer - N_A_LAYERS
            x = fox_layer(x, k_all, v_all, dq, dkT, q_off, g_norm_b[j], w_in_b[j], w_out_b[j])
        x = ple_add(x, p[layer], w_ple_in[layer], g_ple[layer], w_ple_gate[layer])
        if layer == N_A_LAYERS - 1:
            k_new, v_new, logf_new = shared_kv(x, g_kv, w_kv, b_f)
            if past_k is None:
                k_all, v_all, logf_all, q_off = k_new, v_new, logf_new, 0
            else:
                k_all = jnp.concatenate([past_k.astype(k_new.dtype), k_new], axis=1)
                v_all = jnp.concatenate([past_v.astype(v_new.dtype), v_new], axis=1)
                logf_all = jnp.concatenate([past_logf.astype(jnp.float32), logf_new], axis=1)
                q_off = past_k.shape[1]
            dcum = jnp.cumsum(logf_all, axis=1)
            dq = dcum[:, q_off:]
            dkT = jnp.transpose(dcum, (0, 2, 1))
    y = rmsnorm(x, g_final)
    return y, jnp.stack(hgrn_states), k_new, v_new, logf_new


def setup_inputs(seed: int = 0) -> dict:
    key = jax.random.key(seed)
    ks = jax.random.split(key, 32)
    f32 = jnp.float32

    def nrm(k, shape, scale):
        return jax.random.normal(k, shape, f32) * scale

    D = D_MODEL
    return {
        "x_prompt": nrm(ks[0], (BATCH, SEQ, D), 1.0),
        "x_sample": nrm(ks[1], (DEC_BATCH, DEC_SEQ, D), 1.0),
        "state_hgrn": nrm(ks[2], (N_A_LAYERS, DEC_BATCH, H_A, K_A, V_A), 0.5),
        "cache_k": nrm(ks[3], (DEC_BATCH, PAST_LEN, H_B, D_HEAD_B), 1.0),
        "cache_v": nrm(ks[4], (DEC_BATCH, PAST_LEN, H_B, D_HEAD_B), 1.0),
        "cache_logf": jax.nn.log_sigmoid(FORGET_BIAS_INIT + nrm(ks[5], (DEC_BATCH, PAST_LEN, H_B), 1.0)),
        "p_prompt": nrm(ks[6], (DEPTH, BATCH, SEQ, D_PLE), 1.0),
        "p_sample": nrm(ks[7], (DEPTH, DEC_BATCH, DEC_SEQ, D_PLE), 1.0),
        "g_norm_a": 1.0 + nrm(ks[8], (N_A_LAYERS, D), 0.01),
        "w_in_a": nrm(ks[9], (N_A_LAYERS, D, 4 * D), D ** -0.5),
        "lb_logits": nrm(ks[10], (N_A_LAYERS, D), 1.0),
        "g_out_a": 1.0 + nrm(ks[11], (N_A_LAYERS, D), 0.01),
        "w_out_a": nrm(ks[12], (N_A_LAYERS, D, D), D ** -0.5),
        "g_kv": 1.0 + nrm(ks[13], (D,), 0.01),
        "w_kv": nrm(ks[14], (D, 2 * D + H_B), D ** -0.5),
        "b_f": FORGET_BIAS_INIT + nrm(ks[15], (H_B,), 0.1),
        "g_norm_b": 1.0 + nrm(ks[16], (N_B_LAYERS, D), 0.01),
        "w_in_b": nrm(ks[17], (N_B_LAYERS, D, 2 * D), D ** -0.5),
        "w_out_b": nrm(ks[18], (N_B_LAYERS, D, D), D ** -0.5),
        "w_ple_in": nrm(ks[19], (DEPTH, D_PLE, D), D_PLE ** -0.5),
        "g_ple": 1.0 + nrm(ks[20], (DEPTH, D), 0.01),
        "w_ple_gate": nrm(ks[21], (DEPTH, D, D), D ** -0.5),
        "g_final": 1.0 + nrm(ks[22], (D,), 0.01),
    }


def reference(x_prompt, x_sample, state_hgrn, cache_k, cache_v, cache_logf, p_prompt, p_sample,
              g_norm_a, w_in_a, lb_logits, g_out_a, w_out_a, g_kv, w_kv, b_f,
              g_norm_b, w_in_b, w_out_b, w_ple_in, g_ple, w_ple_gate, g_final):
    s0_prompt = jnp.zeros((N_A_LAYERS, x_prompt.shape[0], H_A, K_A, V_A), jnp.float32)
    y_prompt, state_hgrn_prompt, k_prompt, v_prompt, logf_prompt = trunk(
        x_prompt, p_prompt, s0_prompt, None, None, None,
        g_norm_a, w_in_a, lb_logits, g_out_a, w_out_a, g_kv, w_kv, b_f,
        g_norm_b, w_in_b, w_out_b, w_ple_in, g_ple, w_ple_gate, g_final)
    y_sample, state_hgrn_sample, k_sample, v_sample, logf_sample = trunk(
        x_sample, p_sample, state_hgrn, cache_k, cache_v, cache_logf,
        g_norm_a, w_in_a, lb_logits, g_out_a, w_out_a, g_kv, w_kv, b_f,
        g_norm_b, w_in_b, w_out_b, w_ple_in, g_ple, w_ple_gate, g_final)
    return (y_prompt, y_sample, state_hgrn_prompt, state_hgrn_sample,
            k_prompt, v_prompt, logf_prompt, k_sample, v_sample, logf_sample)
```

```python
import contextlib
import numpy as np
import concourse.bass as bass
import concourse.mybir as mybir
from concourse.bass_utils import run_bass_kernel_spmd

F32 = mybir.dt.float32
BF16 = mybir.dt.bfloat16
ACT = mybir.ActivationFunctionType
ALU = mybir.AluOpType

D = 2048
KC = 16
T = 16384
TW = 512
NT = T // TW
G = 4
NM = NT // G
H = 16
DH = 128
PAST = 1024
SW = 128
EPS = 1e-6
NEG = -30000.0


class Sem:
    def __init__(self, nc, stack, name):
        self.h = stack.enter_context(nc.semaphore(name))
        self.cnt = 0


class Buf:
    def __init__(self, name, ap):
        self.name = name
        self.ap = ap
        self.w = None
        self.r = {}
        self.wsem = None
        self.rsem = None

    def __getitem__(self, k):
        return self.ap[k]


class Eng:
    def __init__(self, fw, name, selfwait):
        self.fw = fw
        self.name = name
        self.sem = Sem(fw.nc, fw.stack, "s_" + name)
        self.seen = {}
        self.selfwait = selfwait
        self.prog = []

    def _wait(self, sem, val):
        if sem is self.sem and not self.selfwait:
            return
        if self.seen.get(sem, 0) >= val:
            return
        h = sem.h
        self.prog.append(lambda e: e.wait_ge(h, val))
        self.seen[sem] = val

    def deps(self, reads, writes):
        for b in reads:
            if b.w is not None:
                self._wait(*b.w)
        for b in writes:
            if b.w is not None:
                self._wait(*b.w)
            for s, v in b.r.items():
                self._wait(s, v)

    def op(self, fn, reads=(), writes=(), inc=True):
        self.deps(reads, writes)
        if inc:
            self.sem.cnt += 1
            h = self.sem.h
            self.prog.append(lambda e: fn(e).then_inc(h, 1))
            mark = self.sem.cnt
        else:
            self.prog.append(fn)
            mark = self.sem.cnt + 1
        for b in reads:
            b.r[self.sem] = mark
        for b in writes:
            b.w = (self.sem, mark)
            b.r = {}

    def dma(self, out_ap, in_ap, reads=(), writes=()):
        self.deps(reads, writes)
        fw = self.fw
        if writes:
            b = writes[0]
            if b.wsem is None:
                b.wsem = Sem(fw.nc, fw.stack, "w_" + b.name)
            s = b.wsem
        else:
            b = reads[0]
            if b.rsem is None:
                b.rsem = Sem(fw.nc, fw.stack, "r_" + b.name)
            s = b.rsem
        s.cnt += 16
        sh = s.h
        self.prog.append(lambda e: e.dma_start(out=out_ap, in_=in_ap).then_inc(sh, 16))
        for b in reads:
            b.r[s] = s.cnt
        for b in writes:
            b.w = (s, s.cnt)
            b.r = {}


class FW:
    def __init__(self, nc, stack):
        self.nc = nc
        self.stack = stack
        self.pe = Eng(self, "pe", False)
        self.act = Eng(self, "act", True)
        self.dve = Eng(self, "dve", True)
        self.pool = Eng(self, "pool", True)
        self.sp = Eng(self, "sp", False)
        self.nb = 0

    def sb(self, name, shape, dt):
        return Buf(name, self.stack.enter_context(self.nc.sbuf_tensor(name, list(shape), dt)))

    def ps(self, name):
        return Buf(name, self.stack.enter_context(self.nc.psum_tensor(name, [128, 512], F32)))

    def dram(self, name, shape, dt, kind="Internal"):
        return Buf(name, self.nc.dram_tensor(name, list(shape), dt, kind=kind).ap())

    def emit(self):
        with self.nc.Block() as block:
            @block.tensor
            def _(e):
                for f in self.pe.prog:
                    f(e)

            @block.scalar
            def _(e):
                for f in self.act.prog:
                    f(e)

            @block.vector
            def _(e):
                for f in self.dve.prog:
                    f(e)

            @block.gpsimd
            def _(e):
                for f in self.pool.prog:
                    f(e)

            @block.sync
            def _(e):
                for f in self.sp.prog:
                    f(e)


def build():
    nc = bass.Bass("TRN2", target_bir_lowering=False)
    st = contextlib.ExitStack()
    fw = FW(nc, st)
    pe, act, dve, pool, sp = fw.pe, fw.act, fw.dve, fw.pool, fw.sp

    def inp(name, shape):
        return fw.dram(name, shape, F32, kind="ExternalInput")

    def outp(name, shape):
        return fw.dram(name, shape, F32, kind="ExternalOutput")

    xT_in = inp("xT", [D, T])
    pT_in = inp("pT", [4, 256, T])
    pTo_in = inp("pTo", [2, 256, NM * TW])
    xsT_in = inp("xsT", [D, SW])
    psT_in = inp("psT", [4, 256, SW])
    st_in = inp("st", [2, 2, H, 128, 128])
    ckT_in = inp("ckT", [2, H, 128, PAST])
    cv_in = inp("cv", [2, H, 128, 8, 128])
    clf_in = inp("clf", [2, 128, 8, H])
    w_in_a = inp("w_in_a", [2, D, 4 * D])
    w_out_a = inp("w_out_a", [2, D, D])
    w_kv = inp("w_kv", [D, 2 * D + H])
    w_in_b = inp("w_in_b", [2, D, 2 * D])
    w_out_b = inp("w_out_b", [2, D, D])
    w_pin = inp("w_pin", [4, 256, D])
    w_pg = inp("w_pg", [4, D, D])
    gcols = inp("gcols", [128, 14, KC])
    bf_in = inp("bf", [128, H])
    cst = inp("cst", [128, 128 + 128 + 512 + 128 + 128])
    selc = inp("selc", [128, 4 * G])
    amask = inp("amask", [4, 128, TW])
    smask = inp("smask", [128, 64])

    yT_out = outp("yT", [D, NM * TW])
    ysT_out = outp("ysT", [D, SW])
    stp_out = outp("stp", [2, H, 128, 128])
    sts_out = outp("sts", [2, 2, H, 128, 128])
    kT_out = outp("kT", [D, T])
    v_out = outp("v", [T, D])
    lf_out = outp("lf", [T, H])
    ksT_out = outp("ksT", [D, SW])
    vs_out = outp("vs", [SW, D])
    lfs_out = outp("lfs", [SW, H])

    WA = fw.dram("WA", [2, H, D, 512], BF16)
    WOA = fw.dram("WOA", [2, D, D], BF16)
    WKV = fw.dram("WKV", [D, 2 * D + H], BF16)
    WIB = fw.dram("WIB", [2, D, 2 * D], BF16)
    WOB = fw.dram("WOB", [2, D, D], BF16)
    WPI = fw.dram("WPI", [4, 256, D], BF16)
    WPG = fw.dram("WPG", [4, D, D], BF16)
    KTS = fw.dram("KTS", [H, 128, T], BF16)
    VS = fw.dram("VS", [H, 128, T // 128, 128], BF16)

    sbt = fw.sb
    xT = sbt("xTt", [128, KC, TW], F32)
    xTb = xT.ap.bitcast(BF16)
    xn = sbt("xn", [128, KC, TW], BF16)
    sq4 = sbt("sq4", [128, 4, TW], BF16)
    u = sbt("u", [128, KC, TW], BF16)
    acc = sbt("acc", [128, KC, TW], F32)
    rstd = sbt("rstd", [128, TW], F32)
    wb = [sbt(f"wb{i}", [128, KC, 256], BF16) for i in range(2)]
    pt32 = sbt("pt32", [128, 2, TW], F32)
    ptb = sbt("ptb", [128, 2, TW], BF16)
    wpi = [sbt(f"wpi{i}", [128, 2, 256], BF16) for i in range(2)]
    tq = sbt("tq", [128, TW], F32)
    tf = sbt("tf", [128, TW], F32)
    tg = sbt("tg", [128, TW], F32)
    tG = sbt("tG", [128, TW], F32)
    tE2 = sbt("tE2", [128, TW], F32)
    gsl = sbt("gsl", [128, TW], F32)
    t1 = sbt("t1", [128, TW], F32)
    qd = sbt("qd", [128, TW], BF16)
    kd = sbt("kd", [128, TW], BF16)
    kg = sbt("kg", [128, TW], BF16)
    qg = sbt("qg", [128, TW], BF16)
    vt = sbt("vt", [128, TW // 128, 128], BF16)
    at = sbt("at", [128, 128], BF16)
    kgT = sbt("kgT", [128, 128], BF16)
    eL = sbt("eL", [128, 8], F32)
    eM = sbt("eM", [128, 8], F32)
    dec = sbt("dec", [128, 8], F32)
    osq = sbt("osq", [128, TW], BF16)
    S = sbt("S", [128, 2 * H, 128], F32)
    Sbt = sbt("Sbt", [128, 128], BF16)
    Sx = sbt("Sx", [128, 2, 128], F32)
    Sxb = sbt("Sxb", [128, 2, 128], BF16)
    stg = [sbt(f"stg{i}", [128, 512], F32) for i in range(2)]
    stgb = [sbt(f"stgb{i}", [128, 512], BF16) for i in range(2)]
    lf = sbt("lfm", [128, 4, H], F32)
    lty = sbt("lty", [128, H], F32)
    carry = sbt("carry", [128, H], F32)
    cacc = sbt("cacc", [128, H], F32)
    NBLK = max(T // 128, 128)
    dk = sbt("dk", [128, NBLK, H], F32)
    biash = sbt("biash", [128, NBLK], F32)
    ktl = [sbt(f"ktl{i}", [128, 2 * TW], BF16) for i in range(2)]
    vtl = [sbt(f"vtl{i}", [128, 8, 128], BF16) for i in range(2)]
    Pb = [sbt(f"Pb{i}", [128, TW], BF16) for i in range(3)]
    amk = sbt("amk", [128, 4, TW], BF16)
    mtmp = sbt("mtmp", [128, TW], BF16)
    gc = sbt("gc", [128, 14, KC], F32)
    lbc = sbt("lbc", [128, 2, KC], F32)
    omlb = sbt("omlb", [128, 2, KC], F32)
    bfc = sbt("bfc", [128, H], F32)
    csts = sbt("csts", [128, 1024], F32)
    mask2 = csts.ap[:, 0:128]
    UT = csts.ap[:, 128:256]
    rmask = csts.ap[:, 256:768]
    onesf = csts.ap[:, 896:1024]
    identb = sbt("identb", [128, 128], BF16)
    onesb = sbt("onesb", [128, 128], BF16)
    ones128 = sbt("ones128", [128, 128], BF16)
    ones2048 = sbt("ones2048", [128, 128], BF16)
    sel = sbt("sel", [128, 4 * G], F32)
    smk = sbt("smk", [128, 64], F32)
    epsc = sbt("epsc", [128, 2], F32)
    dkf = dk.ap.rearrange("p b h -> p (b h)")
    cst8 = dkf[:, 0:1024]
    clf = dkf[:, 1024:1024 + 256].rearrange("p (s b h) -> p s b h", s=2, b=8)
    dks = dkf[:, 1280:1280 + 288].rearrange("p (s b h) -> p s b h", s=2, b=9)
    bss = dkf[:, 1568:1568 + 288].rearrange("p (s b h) -> p s b h", s=2, b=9)
    cs = dkf[:, 1856:1856 + 32].rearrange("p (s h) -> p s h", s=2)
    ckb = ktl[0].ap
    cvb = vtl[0].ap
    ksn = amk.ap.rearrange("p a (b c) -> p (a b) c", c=128)
    vsn = S.ap.bitcast(BF16)[:, 0:8, :].rearrange("p a b -> p (a b)")

    banks = [fw.ps(f"bk{i}") for i in range(8)]
    rot = [0]

    def bank():
        b = banks[rot[0] % 4]
        rot[0] += 1
        return b

    pend = []

    def sp_store(out_ap, in_ap, reads, writes):
        pend.append((out_ap, in_ap, reads, writes))

    def flush():
        for p in pend:
            sp.dma(*p)
        pend.clear()

    def mm(pb, out_ap, lhsT, rhs, reads, start, stop, last):
        pe.op(lambda e: e.matmul(out_ap, lhsT=lhsT, rhs=rhs, start=start, stop=stop, skip_group_check=True),
              reads, [pb], inc=last)

    def load_consts():
        sp.dma(gc[:], gcols[:], [gcols], [gc])
        sp.dma(bfc[:], bf_in[:], [bf_in], [bfc])
        sp.dma(csts[:], cst[:], [cst], [csts])
        sp.dma(sel[:], selc[:], [selc], [sel])
        sp.dma(smk[:], smask[:], [smask], [smk])
        dve.op(lambda e: e.tensor_copy(out=identb[:], in_=csts.ap[:, 768:896]), [csts], [identb])
        dve.op(lambda e: e.tensor_copy(out=onesb[:], in_=onesf), [csts], [onesb])
        dve.op(lambda e: e.tensor_scalar(out=ones128[:], in0=onesf, scalar1=1.0 / 128, scalar2=None, op0=ALU.mult), [csts], [ones128])
        dve.op(lambda e: e.tensor_scalar(out=ones2048[:], in0=onesf, scalar1=1.0 / 2048, scalar2=None, op0=ALU.mult), [csts], [ones2048])
        dve.op(lambda e: e.memset(lbc[:, 0, :], 0.0), [], [lbc])
        dve.op(lambda e: e.memset(epsc[:, 0:1], EPS), [], [epsc])
        dve.op(lambda e: e.memset(epsc[:, 1:2], 1.0), [], [epsc])
        dve.op(lambda e: e.tensor_tensor(out=lbc[:, 1, :], in0=gc[:, 13, :], in1=gc[:, 12, :], op=ALU.subtract), [gc], [lbc])
        act.op(lambda e: e.activation(out=lbc[:, 1, :], in_=lbc[:, 1, :], func=ACT.Sigmoid), [lbc], [lbc])
        dve.op(lambda e: e.tensor_scalar(out=omlb[:], in0=lbc[:], scalar1=-1.0, scalar2=1.0, op0=ALU.mult, op1=ALU.add), [lbc], [omlb])

    def prompt_setup():
        for i in range(4):
            s32 = stg[i % 2]
            sp.dma(s32[:], amask[i], [amask], [s32])
            dve.op(lambda e, i=i, s32=s32: e.tensor_copy(out=amk[:, i, :], in_=s32[:]), [s32], [amk])
        dve.op(lambda e: e.memset(carry[:], 0.0), [], [carry])
        dve.op(lambda e: e.memset(S[:], 0.0), [], [S])

    def precast(src_ap, dst_buf, dst_ap, gidx, rows, cols, headmajor=False):
        nkc = rows // 128
        for kc in range(nkc):
            for c0 in range(0, cols, 512):
                cw = min(512, cols - c0)
                s32 = stg[(kc + c0 // 512) % 2]
                s16 = stgb[(kc + c0 // 512) % 2]
                sp.dma(s32[:, :cw], src_ap[kc * 128:(kc + 1) * 128, c0:c0 + cw], [], [s32])
                if gidx is None:
                    act.op(lambda e, s32=s32, s16=s16, cw=cw: e.activation(out=s16[:, :cw], in_=s32[:, :cw], func=ACT.Copy), [s32], [s16])
                else:
                    act.op(lambda e, s32=s32, s16=s16, cw=cw, kc=kc: e.activation(out=s16[:, :cw], in_=s32[:, :cw], func=ACT.Identity, scale=gc[:, gidx, kc:kc + 1]), [s32, gc], [s16])
                if headmajor:
                    act.dma(dst_ap(kc, c0, cw), s16.ap.rearrange("p (h f) -> p h f", f=128), [s16], [dst_buf])
                else:
                    act.dma(dst_ap(kc, c0, cw), s16[:, :cw], [s16], [dst_buf])

    def precast_all():
        for l in range(2):
            def dsta(kc, c0, cw, l=l):
                typ, hh = divmod(c0 // 128, H)
                return WA.ap[l, hh:hh + 4, kc * 128:(kc + 1) * 128, typ * 128:(typ + 1) * 128].rearrange("h p f -> p h f")
            precast(w_in_a.ap[l], WA, dsta, 0 + l, D, 4 * D, headmajor=True)
            precast(w_out_a.ap[l], WOA, lambda kc, c0, cw, l=l: WOA.ap[l, kc * 128:(kc + 1) * 128, c0:c0 + cw], 2 + l, D, D)
            precast(w_in_b.ap[l], WIB, lambda kc, c0, cw, l=l: WIB.ap[l, kc * 128:(kc + 1) * 128, c0:c0 + cw], 5 + l, D, 2 * D)
            precast(w_out_b.ap[l], WOB, lambda kc, c0, cw, l=l: WOB.ap[l, kc * 128:(kc + 1) * 128, c0:c0 + cw], None, D, D)
        precast(w_kv.ap, WKV, lambda kc, c0, cw: WKV.ap[kc * 128:(kc + 1) * 128, c0:c0 + cw], 4, D, 2 * D + H)
        for l in range(4):
            precast(w_pin.ap[l], WPI, lambda kc, c0, cw, l=l: WPI.ap[l, kc * 128:(kc + 1) * 128, c0:c0 + cw], None, 256, D)
            precast(w_pg.ap[l], WPG, lambda kc, c0, cw, l=l: WPG.ap[l, kc * 128:(kc + 1) * 128, c0:c0 + cw], 7 + l, D, D)

    wslot = [0]

    def load_w(src_buf, src_ap):
        w_ = wb[wslot[0] % 2]
        wslot[0] += 1
        ncols = src_ap.shape[1]
        sp.dma(w_[:, :, :ncols], src_ap.rearrange("(kc p) n -> p kc n", p=128), [src_buf], [w_])
        return w_

    def rmsnorm(xt, w):
        pb = bank()
        for g in range(4):
            act.op(lambda e, g=g: e.activation(out=sq4[:, :, :w], in_=xt[:, 4 * g:4 * g + 4, :w], func=ACT.Square), [xt], [sq4])
            for k in range(4):
                mm(pb, pb[:, :w], ones2048[:], sq4[:, k, :w], [ones2048, sq4], g == 0 and k == 0, g == 3 and k == 3, k == 3)
        if 'no_ln' in SKIP:
            return
        act.op(lambda e: e.activation(out=rstd[:, :w], in_=pb[:, :w], func=ACT.Ln, bias=epsc[:, 0:1]), [pb, epsc], [rstd])
        act.op(lambda e: e.activation(out=rstd[:, :w], in_=rstd[:, :w], func=ACT.Exp, scale=-0.5), [rstd], [rstd])
        if 'no_mul' in SKIP:
            return
        for kc in range(KC):
            eng = dve if (kc % 2 == 0 or 'no_pool' in SKIP) else pool
            eng.op(lambda e, kc=kc: e.tensor_tensor(out=xn[:, kc, :w], in0=xt[:, kc, :w], in1=rstd[:, :w], op=ALU.mult), [xt, rstd], [xn])

    def proj_fm(src_buf, src_ap_fn, nfo, rhs_buf, rhs_fn, w, epi, pre=None):
        for g0 in range(0, nfo, 2):
            w_ = load_w(src_buf, src_ap_fn(g0))
            if pre is not None:
                pre(g0)
            for fo in range(g0, g0 + 2):
                pb = bank()
                for kc in range(KC):
                    mm(pb, pb[:, :w], w_[:, kc, (fo - g0) * 128:(fo - g0 + 1) * 128], rhs_fn(kc), [w_, rhs_buf], kc == 0, kc == KC - 1, kc == KC - 1)
                epi(fo, pb)

    def out_proj(WS, l, xt, w):
        def epi(fo, pb):
            dve.op(lambda e: e.tensor_tensor(out=xt[:, fo, :w], in0=xt[:, fo, :w], in1=pb[:, :w], op=ALU.add), [xt, pb], [xt])
        proj_fm(WS, lambda g0: WS.ap[l][:, g0 * 128:(g0 + 2) * 128], KC, u, lambda kc: u[:, kc, :w], w, epi)

    wpslot = [0]
    cur = {}

    def ple(l, xt, w, p_buf, p_ap):
        rmsnorm(xt, w)
        sp.dma(pt32[:, :, :w], p_ap.rearrange("(c p) t -> p c t", p=128), [p_buf], [pt32])
        dve.op(lambda e: e.tensor_copy(out=ptb[:, :, :w], in_=pt32[:, :, :w]), [pt32], [ptb])

        def pre(g0):
            wp = wpi[wpslot[0] % 2]
            wpslot[0] += 1
            sp.dma(wp[:], WPI.ap[l][:, g0 * 128:(g0 + 2) * 128].rearrange("(c p) n -> p c n", p=128), [WPI], [wp])
            cur["wp"] = wp
            cur["g0"] = g0

        def epi(fo, pb):
            if 'ple_noepi' in SKIP:
                dve.op(lambda e: e.tensor_tensor(out=xt[:, fo, :w], in0=xt[:, fo, :w], in1=pb[:, :w], op=ALU.add), [xt, pb], [xt])
                return
            wp, g0 = cur["wp"], cur["g0"]
            if 'ple_nomm' in SKIP:
                act.op(lambda e: e.activation(out=t1[:, :w], in_=pb[:, :w], func=ACT.Sigmoid), [pb], [t1])
                dve.op(lambda e: e.tensor_tensor(out=xt[:, fo, :w], in0=xt[:, fo, :w], in1=t1[:, :w], op=ALU.add), [xt, t1], [xt])
                return
            pe_ = bank()
            for c in range(2):
                mm(pe_, pe_[:, :w], wp[:, c, (fo - g0) * 128:(fo - g0 + 1) * 128], ptb[:, c, :w], [wp, ptb], c == 0, c == 1, c == 1)
            act.op(lambda e: e.activation(out=t1[:, :w], in_=pb[:, :w], func=ACT.Sigmoid), [pb], [t1])
            if 'ple_nodve' not in SKIP:
                dve.op(lambda e: e.tensor_tensor(out=t1[:, :w], in0=t1[:, :w], in1=pe_[:, :w], op=ALU.mult), [t1, pe_], [t1])
            dve.op(lambda e: e.tensor_tensor(out=xt[:, fo, :w], in0=xt[:, fo, :w], in1=t1[:, :w], op=ALU.add), [xt, t1], [xt])
        proj_fm(WPG, lambda g0: WPG.ap[l][:, g0 * 128:(g0 + 2) * 128], KC, xn, lambda kc: xn[:, kc, :w], w, epi, None if 'ple_nopre' in SKIP else pre)

    def hgrn(l, xt, w, sample):
        rmsnorm(xt, w)
        nch = w // 64
        npair = w // 128
        lbv = lbc.ap.rearrange("p l k -> p (l k)")
        omv = omlb.ap.rearrange("p l k -> p (l k)")
        for h in range(H):
            wa = load_w(WA, WA.ap[l, h][:, 0:256])
            wa2 = load_w(WA, WA.ap[l, h][:, 256:512])
            li = l * KC + h
            pq, pf, pg, pv = bank(), bank(), bank(), bank()
            for kc in range(KC):
                mm(pq, pq[:, :w], wa[:, kc, 0:128], xn[:, kc, :w], [wa, xn], kc == 0, kc == KC - 1, kc == KC - 1)
            for kc in range(KC):
                mm(pf, pf[:, :w], wa[:, kc, 128:256], xn[:, kc, :w], [wa, xn], kc == 0, kc == KC - 1, kc == KC - 1)
            for kc in range(KC):
                mm(pg, pg[:, :w], wa2[:, kc, 128:256], xn[:, kc, :w], [wa2, xn], kc == 0, kc == KC - 1, kc == KC - 1)
            for blk in range(npair):
                for kc in range(KC):
                    mm(pv, pv[:, blk * 128:(blk + 1) * 128], xn[:, kc, blk * 128:(blk + 1) * 128], wa2[:, kc, 0:128], [wa2, xn],
                       kc == 0, kc == KC - 1, kc == KC - 1 and blk == npair - 1)
            act.op(lambda e, pq=pq: e.activation(out=tq[:, :w], in_=pq[:, :w], func=ACT.Silu), [pq], [tq])
            act.op(lambda e, pg=pg: e.activation(out=gsl[:, :w], in_=pg[:, :w], func=ACT.Silu), [pg], [gsl])
            act.op(lambda e, pf=pf: e.activation(out=tf[:, :w], in_=pf[:, :w], func=ACT.Sigmoid), [pf], [tf])
            act.op(lambda e, pv=pv: e.activation(out=vt[:, :npair, :], in_=pv[:, :w].rearrange("p (b v) -> p b v", v=128), func=ACT.Copy), [pv], [vt])
            if 'h1' in SKIP:
                continue
            dve.op(lambda e, li=li: e.tensor_scalar(out=tf[:, :w], in0=tf[:, :w], scalar1=omv[:, li:li + 1], scalar2=lbv[:, li:li + 1], op0=ALU.mult, op1=ALU.add), [tf, lbc, omlb], [tf])
            act.op(lambda e: e.activation(out=tg[:, :w], in_=tf[:, :w], func=ACT.Ln), [tf], [tg])
            dve.op(lambda e: e.tensor_scalar(out=tf[:, :w], in0=tf[:, :w], scalar1=-1.0, scalar2=1.0, op0=ALU.mult, op1=ALU.add), [tf], [tf])
            if 'h2' in SKIP:
                continue
            dve.op(lambda e: e.tensor_tensor_scan(out=tG[:, :w], data0=rmask[:, :w], data1=tg[:, :w], initial=0.0, op0=ALU.mult, op1=ALU.add), [tg, csts], [tG])
            G3 = tG.ap[:, :w].rearrange("p (c t) -> p c t", t=64)
            D3 = tg.ap[:, :w].rearrange("p (c t) -> p c t", t=64)
            dve.op(lambda e: e.tensor_tensor(out=D3, in0=G3, in1=G3[:, :, 31:32].to_broadcast([128, nch, 64]), op=ALU.subtract), [tG], [tg])
            if 'h3' in SKIP:
                continue
            act.op(lambda e: e.activation(out=t1[:, :w], in_=tg[:, :w], func=ACT.Exp), [tg], [t1])
            act.op(lambda e: e.activation(out=tE2[:, :w], in_=tg[:, :w], func=ACT.Exp, scale=-1.0), [tg], [tE2])
            act.op(lambda e: e.activation(out=eL[:, :nch], in_=D3[:, :, 63], func=ACT.Exp), [tg], [eL])
            act.op(lambda e: e.activation(out=eM[:, :nch], in_=G3[:, :, 31], func=ACT.Exp), [tG], [eM])
            act.op(lambda e: e.activation(out=tG[:, :w], in_=tG[:, :w], func=ACT.Exp), [tG], [tG])
            dve.op(lambda e: e.tensor_tensor(out=dec[:, :nch], in0=eL[:, :nch], in1=eM[:, :nch], op=ALU.mult), [eL, eM], [dec])
            dve.op(lambda e: e.tensor_tensor(out=qd[:, :w], in0=tq[:, :w], in1=t1[:, :w], op=ALU.mult), [tq, t1], [qd])
            dve.op(lambda e: e.tensor_tensor(out=tE2[:, :w], in0=tf[:, :w], in1=tE2[:, :w], op=ALU.mult), [tf, tE2], [tE2])
            pool.op(lambda e: e.tensor_copy(out=kd[:, :w], in_=tE2[:, :w]), [tE2], [kd])
            K3 = tE2.ap[:, :w].rearrange("p (c t) -> p c t", t=64)
            pool.op(lambda e: e.tensor_tensor(out=kg.ap[:, :w].rearrange("p (c t) -> p c t", t=64), in0=K3, in1=eL[:, :nch].unsqueeze(2).to_broadcast([128, nch, 64]), op=ALU.mult), [tE2, eL], [kg])
            dve.op(lambda e: e.tensor_tensor(out=qg[:, :w], in0=tq[:, :w], in1=tG[:, :w], op=ALU.mult), [tq, tG], [qg])
            if 'h4' in SKIP:
                continue
            ob = banks[4 + h % 2]
            if not sample:
                act.op(lambda e, l=l, h=h: e.activation(out=Sbt[:], in_=S[:, l * H + h, :], func=ACT.Copy), [S], [Sbt])
            for j in range(npair):
                cols = slice(j * 128, (j + 1) * 128)
                pa = bank()
                mm(pa, pa[:, 0:128], kd[:, cols], qd[:, cols], [kd, qd], True, True, True)
                dve.op(lambda e, pa=pa: e.tensor_tensor(out=at[:], in0=pa[:, 0:128], in1=mask2, op=ALU.mult), [pa, csts], [at])
                ptr = bank()
                ptv = ptr.ap.bitcast(BF16)
                pe.op(lambda e, ptv=ptv, cols=cols: e.transpose(out=ptv[:, 0:128], in_=kg[:, cols], identity=identb[:]), [kg, identb], [ptr])
                act.op(lambda e, ptv=ptv: e.activation(out=kgT[:], in_=ptv[:, 0:128], func=ACT.Copy), [ptr], [kgT])
                mm(ob, ob[:, cols], vt[:, j, :], at[:], [vt, at], True, False, False)
                for c in range(2):
                    ch = 2 * j + c
                    ccols = slice(ch * 64, (ch + 1) * 64)
                    if sample:
                        sp.dma(Sx[:, c, :], st_in.ap[l, c, h], [st_in], [Sx])
                        act.op(lambda e, c=c: e.activation(out=Sxb[:, c, :], in_=Sx[:, c, :], func=ACT.Copy), [Sx], [Sxb])
                        s_f, s_b, sbuf_f, sbuf_b = Sx[:, c, :], Sxb[:, c, :], Sx, Sxb
                    else:
                        s_f, s_b, sbuf_f, sbuf_b = S[:, l * H + h, :], Sbt[:], S, Sbt
                    mm(ob, ob[:, ccols], s_b, qg[:, ccols], [sbuf_b, qg], False, True, c == 1)
                    pS = bank()
                    mm(pS, pS[:, 0:128], kgT[c * 64:(c + 1) * 64, :], vt[c * 64:(c + 1) * 64, j, :], [kgT, vt], True, True, True)
                    dve.op(lambda e, s_f=s_f, ch=ch, pS=pS: e.scalar_tensor_tensor(out=s_f, in0=s_f, scalar=dec[:, ch:ch + 1], in1=pS[:, 0:128], op0=ALU.mult, op1=ALU.add), [sbuf_f, dec, pS], [sbuf_f])
                    if sample:
                        sp.dma(sts_out.ap[l, c, h], Sx[:, c, :], [Sx], [sts_out])
                    else:
                        act.op(lambda e, s_f=s_f: e.activation(out=Sbt[:], in_=s_f, func=ACT.Copy), [S], [Sbt])
            if 'h5' in SKIP:
                continue
            act.op(lambda e, ob=ob: e.activation(out=osq[:, :w], in_=ob[:, :w], func=ACT.Square), [ob], [osq])
            if 'h6' in SKIP:
                continue
            pss = bank()
            mm(pss, pss[:, :w], ones128[:], osq[:, :w], [ones128, osq], True, True, True)
            if 'h7' in SKIP:
                continue
            act.op(lambda e, pss=pss: e.activation(out=t1[:, :w], in_=pss[:, :w], func=ACT.Ln, bias=epsc[:, 0:1]), [pss, epsc], [t1])
            act.op(lambda e: e.activation(out=t1[:, :w], in_=t1[:, :w], func=ACT.Exp, scale=-0.5), [t1], [t1])
            if 'h8' in SKIP:
                continue
            dve.op(lambda e, ob=ob: e.tensor_tensor(out=t1[:, :w], in0=t1[:, :w], in1=ob[:, :w], op=ALU.mult), [t1, ob], [t1])
            if 'h9' in SKIP:
                continue
            dve.op(lambda e, h=h: e.tensor_tensor(out=u[:, h, :w], in0=t1[:, :w], in1=gsl[:, :w], op=ALU.mult), [t1, gsl], [u])
        out_proj(WOA, l, xt, w)

    def kv(xt, w, t0, sample):
        rmsnorm(xt, w)
        nblk = w // 128
        ko, vo, lo = (ksT_out, vs_out, lfs_out) if sample else (kT_out, v_out, lf_out)

        def epi(fo, pb):
            s32 = stg[fo % 2]
            act.op(lambda e: e.activation(out=s32[:, :w], in_=pb[:, :w], func=ACT.Copy), [pb], [s32])
            if 'kvA' not in SKIP:
                act.dma(ko.ap[fo * 128:(fo + 1) * 128, t0:t0 + w], s32[:, :w], [s32], [ko])
            if 'kvB' in SKIP:
                return
            if sample:
                dve.op(lambda e: e.tensor_copy(out=ksn[:, fo, :], in_=s32[:, :w]), [s32], [amk])
            else:
                s16 = stgb[fo % 2]
                dve.op(lambda e: e.tensor_copy(out=s16[:, :w], in_=s32[:, :w]), [s32], [s16])
                flush()
                sp_store(KTS.ap[fo, :, t0:t0 + w], s16[:, :w], [s16], [KTS])
        proj_fm(WKV, lambda g0: WKV.ap[:, g0 * 128:(g0 + 2) * 128], KC, xn, lambda kc: xn[:, kc, :w], w, epi)
        if 'kv1' in SKIP:
            flush()
            return
        for g0 in range(8):
            w_ = load_w(WKV, WKV.ap[:, D + g0 * 256:D + (g0 + 1) * 256])
            for blk in range(nblk):
                pb = bank()
                for kc in range(KC):
                    mm(pb, pb[:, 0:256], xn[:, kc, blk * 128:(blk + 1) * 128], w_[:, kc, :], [w_, xn], kc == 0, kc == KC - 1, kc == KC - 1)
                s32 = stg[(g0 * nblk + blk) % 2]
                act.op(lambda e, s32=s32, pb=pb: e.activation(out=s32[:, 0:256], in_=pb[:, 0:256], func=ACT.Copy), [pb], [s32])
                act.dma(vo.ap[t0 + blk * 128:t0 + (blk + 1) * 128, g0 * 256:(g0 + 1) * 256], s32[:, 0:256], [s32], [vo])
                if sample:
                    dve.op(lambda e, s32=s32, g0=g0: e.tensor_copy(out=vsn[:, g0 * 256:(g0 + 1) * 256], in_=s32[:, 0:256]), [s32], [S])
                else:
                    s16 = stgb[(g0 * nblk + blk) % 2]
                    dve.op(lambda e, s16=s16, s32=s32: e.tensor_copy(out=s16[:, 0:256], in_=s32[:, 0:256]), [s32], [s16])
                    gb = t0 // 128 + blk
                    flush()
                    sp_store(VS.ap[g0 * 2:(g0 + 1) * 2, :, gb, :].rearrange("h p d -> p h d"), s16.ap[:, 0:256].rearrange("p (h d) -> p h d", d=128), [s16], [VS])
        if 'kv2' in SKIP:
            flush()
            return
        w_ = load_w(WKV, WKV.ap[:, 2 * D:2 * D + H])
        for blk in range(nblk):
            pb = bank()
            for kc in range(KC):
                mm(pb, pb[:, 0:H], xn[:, kc, blk * 128:(blk + 1) * 128], w_[:, kc, 0:H], [w_, xn], kc == 0, kc == KC - 1, kc == KC - 1)
            dve.op(lambda e, pb=pb: e.tensor_tensor(out=lty[:], in0=pb[:, 0:H], in1=bfc[:], op=ALU.add), [pb, bfc], [lty])
            act.op(lambda e: e.activation(out=lty[:], in_=lty[:], func=ACT.Exp, scale=-1.0), [lty], [lty])
            act.op(lambda e: e.activation(out=lty[:], in_=lty[:], func=ACT.Ln, bias=epsc[:, 1:2]), [lty, epsc], [lty])
            dve.op(lambda e, blk=blk: e.tensor_scalar(out=lf[:, blk, :], in0=lty[:], scalar1=-1.0, scalar2=None, op0=ALU.mult), [lty], [lf])
            if not sample and 'kv3' not in SKIP:
                pc = bank()
                mm(pc, pc[:, 0:H], UT, lf[:, blk, :], [csts, lf], True, True, False)
                mm(pc, pc[:, H:2 * H], onesf, lf[:, blk, :], [csts, lf], True, True, True)
                gb = t0 // 128 + blk
                dve.op(lambda e, pc=pc, gb=gb: e.tensor_tensor(out=dk[:, gb, :], in0=pc[:, 0:H], in1=carry[:], op=ALU.add), [pc, carry], [dk])
                dve.op(lambda e, pc=pc: e.tensor_tensor(out=carry[:], in0=carry[:], in1=pc[:, H:2 * H], op=ALU.add), [pc, carry], [carry])
        if 'kv4' in SKIP:
            flush()
            return
        flush()
        sp.dma(lo.ap[t0:t0 + w, :].rearrange("(b p) h -> p b h", p=128), lf[:, :nblk, :], [lf], [lo])

    def fox_in(lb_, xt, w):
        rmsnorm(xt, w)

        def epq(fo, pb):
            act.op(lambda e: e.activation(out=xTb[:, fo, 0:w], in_=pb[:, :w], func=ACT.Identity, scale=float(DH) ** -0.5), [pb], [xT])

        def epg(fo, pb):
            act.op(lambda e: e.activation(out=xTb[:, fo, TW:TW + w], in_=pb[:, :w], func=ACT.Silu), [pb], [xT])
        proj_fm(WIB, lambda g0: WIB.ap[lb_][:, g0 * 128:(g0 + 2) * 128], KC, xn, lambda kc: xn[:, kc, :w], w, epq)
        proj_fm(WIB, lambda g0: WIB.ap[lb_][:, D + g0 * 128:D + (g0 + 2) * 128], KC, xn, lambda kc: xn[:, kc, :w], w, epg)

    def finish_head(h, ob, db, w, qoff):
        dve.op(lambda e: e.reciprocal(out=t1[:, :w], in_=db[:, :w]), [db], [t1])
        dve.op(lambda e: e.tensor_tensor(out=t1[:, :w], in0=t1[:, :w], in1=ob[:, :w], op=ALU.mult), [t1, ob], [t1])
        dve.op(lambda e: e.tensor_tensor(out=u[:, h, qoff:qoff + w], in0=t1[:, :w], in1=xTb[:, h, TW + qoff:TW + qoff + w], op=ALU.mult), [t1, xT], [u])

    def pv_(ob, db, vt_, i, P, first, last):
        mm(ob, ob[:], vt_[:, i, :], P[:], [vt_, P], first, last, False)
        mm(db, db[:], onesb[:], P[:], [onesb, P], first, last, True)

    def attn_prompt(m):
        nkt = G * (m + 1)
        nbk = nkt * 4
        ld = [0]
        for h in range(H):
            ob, db = banks[4 + h % 2], banks[6 + h % 2]
            dve.op(lambda e, h=h: e.tensor_scalar(out=biash[:, :nbk], in0=dk[:, :nbk, h], scalar1=-1.0, scalar2=cacc[:, h:h + 1], op0=ALU.mult, op1=ALU.add), [dk, cacc], [biash])
            for j in range(G):
                b0 = G * 4 * m + 4 * j
                dve.op(lambda e, b0=b0, j=j: e.tensor_scalar(out=biash[:, b0:b0 + 4], in0=biash[:, b0:b0 + 4], scalar1=sel[:, G + j:G + j + 1], scalar2=sel[:, 2 * G + j:2 * G + j + 1], op0=ALU.mult, op1=ALU.add), [biash, sel], [biash])
            groups = list(range(0, nkt, 2))
            nb = len(groups) * 8

            def load_grp(g2):
                kt_, vt_ = ktl[ld[0] % 2], vtl[ld[0] % 2]
                ld[0] += 1
                sp.dma(kt_[:], KTS.ap[h, :, g2 * TW:(g2 + 2) * TW], [KTS], [kt_])
                sp.dma(vt_[:], VS.ap[h, :, g2 * 4:(g2 + 2) * 4, :], [VS], [vt_])
                return kt_, vt_
            nxt = load_grp(groups[0])
            pend = None
            bi = 0
            for gi, g2 in enumerate(groups):
                kt_, vt_ = nxt
                if gi + 1 < len(groups):
                    if pend is not None:
                        pv_(ob, db, *pend)
                        pend = None
                    nxt = load_grp(groups[gi + 1])
                for i in range(8):
                    gblk = g2 * 4 + i
                    ps_ = bank()
                    mm(ps_, ps_[:], kt_[:, i * 128:(i + 1) * 128], xTb[:, h, 0:TW], [kt_, xT], True, True, True)
                    P = Pb[bi % 3]
                    act.op(lambda e, P=P, ps_=ps_, gblk=gblk: e.activation(out=P[:], in_=ps_[:], func=ACT.Exp, bias=biash[:, gblk:gblk + 1]), [ps_, biash], [P])
                    if gblk >= G * 4 * m:
                        jj, ii = divmod(gblk - G * 4 * m, 4)
                        pool.op(lambda e, ii=ii, jj=jj: e.tensor_scalar(out=mtmp[:], in0=amk[:, ii, :], scalar1=sel[:, jj:jj + 1], scalar2=sel[:, 3 * G + jj:3 * G + jj + 1], op0=ALU.mult, op1=ALU.add), [amk, sel], [mtmp])
                        pool.op(lambda e, P=P: e.tensor_tensor(out=P[:], in0=P[:], in1=mtmp[:], op=ALU.mult), [P, mtmp], [P])
                    if pend is not None:
                        pv_(ob, db, *pend)
                    pend = (vt_, i, P, bi == 0, bi == nb - 1)
                    bi += 1
            pv_(ob, db, *pend)
            finish_head(h, ob, db, TW, 0)

    def attn_sample():
        for s_ in range(2):
            p0 = s_ * 64
            pr = slice(p0, p0 + 64)
            for h in range(H):
                ob, db = banks[4 + h % 2], banks[6 + h % 2]
                sp.dma(cst8, ckT_in.ap[s_, h], [ckT_in], [dk])
                dve.op(lambda e: e.tensor_copy(out=ckb[:], in_=cst8), [dk], [ktl[0]])
                sp.dma(cst8, cv_in.ap[s_, h].rearrange("p b d -> p (b d)"), [cv_in], [dk])
                pool.op(lambda e: e.tensor_copy(out=cvb[:], in_=cst8.rearrange("p (b d) -> p b d", d=128)), [dk], [vtl[0]])
                qcols = slice(p0, p0 + 64)
                for b in range(9):
                    ps_ = bank()
                    P = Pb[b % 3]
                    if b < 8:
                        mm(ps_, ps_[:, 0:64], ckb[:, b * 128:(b + 1) * 128], xTb[:, h, qcols], [ktl[0], xT], True, True, True)
                        act.op(lambda e, P=P, ps_=ps_, b=b, h=h, s_=s_: e.activation(out=P[:, 0:64], in_=ps_[:, 0:64], func=ACT.Exp, bias=bss[:, s_, b, h:h + 1]), [ps_, dk], [P])
                        mm(ob, ob[:, 0:64], cvb[:, b, :], P[:, 0:64], [vtl[0], P], b == 0, False, False)
                        mm(db, db[:, 0:64], onesb[:], P[:, 0:64], [onesb, P], b == 0, False, True)
                    else:
                        mm(ps_, ps_[:, 0:64], ksn[:, h, 0:128], xTb[:, h, qcols], [amk, xT], True, True, True)
                        act.op(lambda e, P=P, ps_=ps_, h=h, s_=s_, pr=pr: e.activation(out=P[pr, 0:64], in_=ps_[pr, 0:64], func=ACT.Exp, bias=bss[pr, s_, 8, h:h + 1]), [ps_, dk], [P])
                        dve.op(lambda e, P=P, pr=pr: e.tensor_tensor(out=P[pr, 0:64], in0=P[pr, 0:64], in1=smk[pr, :], op=ALU.mult), [P, smk], [P])
                        mm(ob, ob[:, 0:64], vsn[pr, h * 128:(h + 1) * 128], P[pr, 0:64], [S, P], False, True, False)
                        mm(db, db[:, 0:64], onesb[pr, :], P[pr, 0:64], [onesb, P], False, True, True)
                finish_head(h, ob, db, 64, p0)

    def sample_dcum():
        for s_ in range(2):
            p0 = s_ * 64
            sp.dma(clf[:, s_, :, :], clf_in.ap[s_], [clf_in], [dk])
            dve.op(lambda e: e.memset(lty[:], 0.0), [], [lty])
            for b in range(9):
                pc = bank()
                if b < 8:
                    mm(pc, pc[:, 0:H], UT, clf[:, s_, b, :], [csts, dk], True, True, False)
                    mm(pc, pc[:, H:2 * H], onesf, clf[:, s_, b, :], [csts, dk], True, True, True)
                    dve.op(lambda e, pc=pc, b=b, s_=s_: e.tensor_tensor(out=dks[:, s_, b, :], in0=pc[:, 0:H], in1=lty[:], op=ALU.add), [pc, lty], [dk])
                else:
                    mm(pc, pc[:, 0:H], csts.ap[p0:p0 + 64, 128:256], lf[p0:p0 + 64, 0, :], [csts, lf], True, True, False)
                    mm(pc, pc[:, H:2 * H], csts.ap[p0:p0 + 64, 896:1024], lf[p0:p0 + 64, 0, :], [csts, lf], True, True, True)
                    dve.op(lambda e, pc=pc, s_=s_, p0=p0: e.tensor_tensor(out=dks[p0:p0 + 64, s_, 8, :], in0=pc[p0:p0 + 64, 0:H], in1=lty[p0:p0 + 64, :], op=ALU.add), [pc, lty], [dk])
                dve.op(lambda e, pc=pc: e.tensor_tensor(out=lty[:], in0=lty[:], in1=pc[:, H:2 * H], op=ALU.add), [pc, lty], [lty])
            dve.op(lambda e, s_=s_: e.tensor_copy(out=cs[:, s_, :], in_=lty[:]), [lty], [dk])
            for b in range(9):
                prr = slice(0, 128) if b < 8 else slice(p0, p0 + 64)
                dve.op(lambda e, b=b, s_=s_, prr=prr: e.tensor_tensor(out=bss[prr, s_, b, :], in0=cs[prr, s_, :], in1=dks[prr, s_, b, :], op=ALU.subtract), [dk], [dk])

    def final_out(xt, w, out_buf, out_ap_fn):
        rmsnorm(xt, w)
        for kc in range(KC):
            s32 = stg[kc % 2]
            dve.op(lambda e, kc=kc, s32=s32: e.tensor_scalar(out=s32[:, :w], in0=xt[:, kc, :w], scalar1=gc[:, 11, kc:kc + 1], scalar2=None, op0=ALU.mult), [xt, gc], [s32])
            dve.op(lambda e, s32=s32: e.tensor_tensor(out=s32[:, :w], in0=s32[:, :w], in1=rstd[:, :w], op=ALU.mult), [s32, rstd], [s32])
            flush()
            sp_store(out_ap_fn(kc), s32[:, :w], [s32], [out_buf])

    _fo = final_out

    def final_out(*a):
        _fo(*a)
        flush()

    load_consts()
    if DO_PRECAST:
        precast_all()

    if DO_SAMPLE:
        w = SW
        sp.dma(acc[:, :, :w], xsT_in.ap.rearrange("(kc p) t -> p kc t", p=128), [xsT_in], [acc])
        for l in range(2):
            hgrn(l, acc, w, True)
            ple(l, acc, w, psT_in, psT_in.ap[l])
        kv(acc, w, 0, True)
        sample_dcum()
        for lb_ in range(2):
            fox_in(lb_, acc, w)
            attn_sample()
            out_proj(WOB, lb_, acc, w)
            ple(2 + lb_, acc, w, psT_in, psT_in.ap[2 + lb_])
        final_out(acc, w, ysT_out, lambda kc: ysT_out.ap[kc * 128:(kc + 1) * 128, :])

    prompt_setup()
    for t in range(NT_RUN):
        m, j = divmod(t, G)
        t0 = t * TW
        sp.dma(xT[:], xT_in.ap[:, t0:t0 + TW].rearrange("(kc p) t -> p kc t", p=128), [xT_in], [xT])
        for l in range(2):
            if 'hgrn' not in SKIP:
                hgrn(l, xT, TW, False)
            if 'ple' not in SKIP:
                ple(l, xT, TW, pT_in, pT_in.ap[l][:, t0:t0 + TW])
        if 'kv' not in SKIP:
            kv(xT, TW, t0, False)
        if 'd_rms' in SKIP:
            rmsnorm(xT, TW)
        if 'd_proj' in SKIP:
            out_proj(WOA, 0, xT, TW)
        for kc in range(KC):
            if j == 0:
                pool.op(lambda e, kc=kc: e.tensor_scalar(out=acc[:, kc, :], in0=xT[:, kc, :], scalar1=sel[:, 0:1], scalar2=None, op0=ALU.mult), [xT, sel], [acc])
            else:
                dve.op(lambda e, kc=kc, j=j: e.scalar_tensor_tensor(out=acc[:, kc, :], in0=xT[:, kc, :], scalar=sel[:, j:j + 1], in1=acc[:, kc, :], op0=ALU.mult, op1=ALU.add), [xT, sel, acc], [acc])
        if j == 0:
            dve.op(lambda e: e.tensor_scalar(out=cacc[:], in0=carry[:], scalar1=sel[:, 0:1], scalar2=None, op0=ALU.mult), [carry, sel], [cacc])
        else:
            dve.op(lambda e, j=j: e.scalar_tensor_tensor(out=cacc[:], in0=carry[:], scalar=sel[:, j:j + 1], in1=cacc[:], op0=ALU.mult, op1=ALU.add), [carry, sel, cacc], [cacc])
        if j == G - 1 and DO_B:
            for lb_ in range(2):
                fox_in(lb_, acc, TW)
                attn_prompt(m)
                out_proj(WOB, lb_, acc, TW)
                ple(2 + lb_, acc, TW, pTo_in, pTo_in.ap[lb_][:, m * TW:(m + 1) * TW])
            final_out(acc, TW, yT_out, lambda kc, m=m: yT_out.ap[kc * 128:(kc + 1) * 128, m * TW:(m + 1) * TW])
    for l in range(2):
        sp.dma(stp_out.ap[l].rearrange("h k v -> k h v"), S[:, l * H:(l + 1) * H, :], [S], [stp_out])

    for ob_ in (yT_out, ysT_out, stp_out, sts_out, kT_out, v_out, lf_out, ksT_out, vs_out, lfs_out):
        if ob_.w is not None:
            sp._wait(*ob_.w)
    fw.emit()
    st.close()
    return nc


DO_SAMPLE = True
DEBUG_CORES = None
SKIP = set()
DO_PRECAST = True
DO_B = True
NT_RUN = NT


def _cols(v):
    return np.ascontiguousarray(v.reshape(KC, 128).T)


def kernel(x_prompt, x_sample, state_hgrn, cache_k, cache_v, cache_logf, p_prompt, p_sample,
           g_norm_a, w_in_a, lb_logits, g_out_a, w_out_a, g_kv, w_kv, b_f,
           g_norm_b, w_in_b, w_out_b, w_ple_in, g_ple, w_ple_gate, g_final):
    f = np.float32
    A = lambda a: np.ascontiguousarray(np.asarray(a, dtype=f))
    x_prompt, x_sample, state_hgrn = A(x_prompt), A(x_sample), A(state_hgrn)
    cache_k, cache_v, cache_logf, p_prompt, p_sample = A(cache_k), A(cache_v), A(cache_logf), A(p_prompt), A(p_sample)
    vecs = [g_norm_a[0], g_norm_a[1], g_out_a[0], g_out_a[1], g_kv, g_norm_b[0], g_norm_b[1],
            g_ple[0], g_ple[1], g_ple[2], g_ple[3], g_final, lb_logits[0], lb_logits[1]]
    gcols = np.ascontiguousarray(np.stack([_cols(A(v)) for v in vecs], axis=1))
    bf = np.ascontiguousarray(np.broadcast_to(A(b_f)[None, :], (128, H)))
    idx = np.arange(128)
    mask2 = ((idx[:, None] // 64 == idx[None, :] // 64) & (idx[:, None] <= idx[None, :])).astype(f)
    UT = (idx[:, None] <= idx[None, :]).astype(f)
    rmask = np.broadcast_to((np.arange(512) % 64 != 0).astype(f)[None, :], (128, 512))
    cst = np.ascontiguousarray(np.concatenate([mask2, UT, rmask, np.eye(128, dtype=f), np.ones((128, 128), f)], axis=1))
    smask = np.ascontiguousarray(np.tile((np.arange(64)[:, None] <= np.arange(64)[None, :]).astype(f), (2, 1)))
    common = dict(w_in_a=A(w_in_a), w_out_a=A(w_out_a), w_kv=A(w_kv), w_in_b=A(w_in_b), w_out_b=A(w_out_b),
                  w_pin=A(w_ple_in), w_pg=A(w_ple_gate), gcols=gcols, bf=bf, cst=cst, smask=smask)
    xT_b = [np.ascontiguousarray(x_prompt[b, :T].T) for b in range(2)]
    pT_b = [np.ascontiguousarray(p_prompt[:, b, :T].transpose(0, 2, 1)) for b in range(2)]
    in_maps = []
    for c in range(8):
        b, r = divmod(c, G)
        own = np.concatenate([np.arange((G * m + r) * TW, (G * m + r + 1) * TW) for m in range(NM)]) % T
        selc = np.zeros((128, 4 * G), f)
        selc[:, r] = 1.0
        for j in range(G):
            selc[:, G + j] = 1.0 if j <= r else 0.0
            selc[:, 2 * G + j] = 0.0 if j <= r else NEG
            selc[:, 3 * G + j] = 0.0 if j == r else 1.0
        am = np.stack([((128 * i + idx)[:, None] <= np.arange(TW)[None, :]).astype(f) for i in range(4)])
        ss = [2 * c, 2 * c + 1]
        d = dict(common)
        d.update(
            xT=xT_b[b], pT=pT_b[b], pTo=np.ascontiguousarray(pT_b[b][2:, :, own]),
            xsT=np.ascontiguousarray(x_sample[ss].reshape(SW, D).T),
            psT=np.ascontiguousarray(p_sample[:, ss].reshape(4, SW, 256).transpose(0, 2, 1)),
            st=np.ascontiguousarray(state_hgrn[:, ss]),
            ckT=np.ascontiguousarray(cache_k[ss].transpose(0, 2, 3, 1)),
            cv=np.ascontiguousarray(cache_v[ss].reshape(2, 8, 128, H, 128).transpose(0, 3, 2, 1, 4)),
            clf=np.ascontiguousarray(cache_logf[ss].reshape(2, 8, 128, H).transpose(0, 2, 1, 3)),
            selc=selc, amask=am)
        in_maps.append(d)
    nc = build()
    cores = DEBUG_CORES if DEBUG_CORES is not None else list(range(8))
    rr = run_bass_kernel_spmd(nc, [in_maps[c] for c in cores], core_ids=list(range(len(cores)))).results
    zero = {n: np.zeros_like(a) for n, a in rr[0].items()}
    res = [rr[cores.index(c)] if c in cores else zero for c in range(8)]
    y_prompt = np.empty((2, T, D), f)
    for c in range(8):
        b, r = divmod(c, G)
        yt = res[c]["yT"].T.reshape(NM, TW, D)
        for m in range(NM):
            tt = G * m + r
            if (tt + 1) * TW <= T:
                y_prompt[b, tt * TW:(tt + 1) * TW] = yt[m]
    y_sample = np.stack([res[c]["ysT"].T.reshape(2, 64, D) for c in range(8)]).reshape(16, 64, D)
    stp = np.stack([res[b * G]["stp"] for b in range(2)], axis=1)
    sts = np.concatenate([res[c]["sts"] for c in range(8)], axis=1)
    k_prompt = np.stack([res[b * G]["kT"].T.reshape(T, H, DH) for b in range(2)])
    v_prompt = np.stack([res[b * G]["v"].reshape(T, H, DH) for b in range(2)])
    lf_prompt = np.stack([res[b * G]["lf"] for b in range(2)])
    k_sample = np.stack([res[c]["ksT"].T.reshape(2, 64, H, DH) for c in range(8)]).reshape(16, 64, H, DH)
    v_sample = np.stack([res[c]["vs"].reshape(2, 64, H, DH) for c in range(8)]).reshape(16, 64, H, DH)
    lf_sample = np.stack([res[c]["lfs"].reshape(2, 64, H) for c in range(8)]).reshape(16, 64, H)
    return (y_prompt, y_sample, np.ascontiguousarray(stp), np.ascontiguousarray(sts),
            k_prompt, v_prompt, lf_prompt, k_sample, v_sample, lf_sample)
```
